# Optimizing a Trainium2 kernel written in Bass

```python
import jax, jax.numpy as jnp
from jax import lax
import numpy as np

D_MODEL = 1024
BATCH = 2
SEQ = 8192
DEPTH = 2

D_PLE = 256
SB_HEADS = 8
SB_HEAD_DIM = 64
SB_WIDTH = SB_HEADS * SB_HEAD_DIM
MLA_HEADS = 4
MLA_NOPE_DIM = 128
MLA_ROPE_DIM = 64
MLA_QK_DIM = MLA_NOPE_DIM + MLA_ROPE_DIM
MLA_V_DIM = 128
MLA_WIDTH = MLA_HEADS * MLA_V_DIM
Q_LORA = 256
KV_LORA = 128
D_MIX = SB_WIDTH + MLA_WIDTH
IN_COLS = 3 * SB_WIDTH + Q_LORA + KV_LORA + MLA_ROPE_DIM
D_FF = 2816
ROPE_THETA = 10000.0
EPS = 1e-6
Q_BLOCK = 128

kernel_name = "hybrid_stickbreak_mla_macaron_ple"


def rms_norm(x, g):
    xf = x.astype(jnp.float32)
    y = xf * lax.rsqrt(jnp.mean(xf * xf, axis=-1, keepdims=True) + EPS)
    return (y * g.astype(jnp.float32)).astype(x.dtype)


def swiglu(x, w_gate, w_up, w_down):
    return (jax.nn.silu(x @ w_gate) * (x @ w_up)) @ w_down


def rope_cos_sin(seq_len):
    half = MLA_ROPE_DIM // 2
    inv_freq = ROPE_THETA ** (-jnp.arange(half, dtype=jnp.float32) / half)
    ang = jnp.arange(seq_len, dtype=jnp.float32)[:, None] * inv_freq[None, :]
    return jnp.cos(ang), jnp.sin(ang)


def apply_rope(x, cos, sin):
    xf = x.astype(jnp.float32)
    x1, x2 = jnp.split(xf, 2, axis=-1)
    out = jnp.concatenate([x1 * cos - x2 * sin, x2 * cos + x1 * sin], axis=-1)
    return out.astype(x.dtype)


def to_heads(t, n_heads):
    b, s, _ = t.shape
    return t.reshape(b, s, n_heads, -1).transpose(0, 2, 1, 3)


def from_heads(t):
    b, h, s, d = t.shape
    return t.transpose(0, 2, 1, 3).reshape(b, s, h * d)


def query_blocks(t):
    b, h, s, d = t.shape
    return t.reshape(b, h, s // Q_BLOCK, Q_BLOCK, d).transpose(2, 0, 1, 3, 4)


def unblock(t):
    nb, b, h, qb, d = t.shape
    return t.transpose(1, 2, 0, 3, 4).reshape(b, h, nb * qb, d)


def stick_breaking_attention(q, k, v):
    s_len, d = q.shape[2], q.shape[3]
    scale = d ** -0.5
    key_pos = jnp.arange(s_len)

    def block(args):
        q_blk, i = args
        q_pos = i * Q_BLOCK + jnp.arange(Q_BLOCK)
        z = jnp.einsum('bhqd,bhkd->bhqk', q_blk, k,
                       preferred_element_type=jnp.float32) * scale
        past = key_pos[None, :] < q_pos[:, None]
        log_one_minus = jnp.where(past, jax.nn.log_sigmoid(-z), 0.0)
        tail = lax.cumsum(log_one_minus, axis=3, reverse=True) - log_one_minus
        a = jnp.where(past, jnp.exp(jax.nn.log_sigmoid(z) + tail), 0.0)
        return jnp.einsum('bhqk,bhkd->bhqd', a.astype(v.dtype), v)

    out = lax.map(block, (query_blocks(q), jnp.arange(s_len // Q_BLOCK)))
    return unblock(out)


def mla_attention(q_nope, q_rope, k_nope, k_rope, v):
    s_len = q_nope.shape[2]
    scale = MLA_QK_DIM ** -0.5
    key_pos = jnp.arange(s_len)
    neg = jnp.finfo(jnp.float32).min

    def block(args):
        qn, qr, i = args
        q_pos = i * Q_BLOCK + jnp.arange(Q_BLOCK)
        s = (jnp.einsum('bhqd,bhkd->bhqk', qn, k_nope, preferred_element_type=jnp.float32)
             + jnp.einsum('bhqr,bkr->bhqk', qr, k_rope, preferred_element_type=jnp.float32)) * scale
        causal = key_pos[None, :] <= q_pos[:, None]
        probs = jax.nn.softmax(jnp.where(causal, s, neg), axis=-1)
        return jnp.einsum('bhqk,bhkd->bhqd', probs.astype(v.dtype), v)

    out = lax.map(block, (query_blocks(q_nope), query_blocks(q_rope),
                          jnp.arange(s_len // Q_BLOCK)))
    return unblock(out)


def hybrid_layer(h, p_i, cos, sin,
                 ffn1_norm, ffn1_w_gate, ffn1_w_up, ffn1_w_down,
                 mix_norm, w_in, q_lat_norm, w_uq, kv_lat_norm, w_ukv,
                 sb_out_norm, mla_out_norm, w_out,
                 ffn2_norm, ffn2_w_gate, ffn2_w_up, ffn2_w_down,
                 ple_norm, w_ple_gate, w_ple_proj):
    b, s, _ = h.shape
    h = h + 0.5 * swiglu(rms_norm(h, ffn1_norm), ffn1_w_gate, ffn1_w_up, ffn1_w_down)

    u = rms_norm(h, mix_norm)
    proj = u @ w_in
    o1 = SB_WIDTH
    o2 = 2 * SB_WIDTH
    o3 = 3 * SB_WIDTH
    o4 = o3 + Q_LORA
    o5 = o4 + KV_LORA
    sb_q, sb_k, sb_v, c_q, c_kv, k_rope = jnp.split(proj, [o1, o2, o3, o4, o5], axis=-1)

    sb_out = stick_breaking_attention(to_heads(sb_q, SB_HEADS), to_heads(sb_k, SB_HEADS),
                                      to_heads(sb_v, SB_HEADS))
    sb_out = rms_norm(from_heads(sb_out), sb_out_norm)

    q = (rms_norm(c_q, q_lat_norm) @ w_uq).reshape(b, s, MLA_HEADS, MLA_QK_DIM)
    q_nope, q_rope = q[..., :MLA_NOPE_DIM], q[..., MLA_NOPE_DIM:]
    q_rope = apply_rope(q_rope, cos[:, None, :], sin[:, None, :])
    kv = (rms_norm(c_kv, kv_lat_norm) @ w_ukv).reshape(b, s, MLA_HEADS, MLA_NOPE_DIM + MLA_V_DIM)
    k_nope, v = kv[..., :MLA_NOPE_DIM], kv[..., MLA_NOPE_DIM:]
    k_rope = apply_rope(k_rope, cos, sin)
    mla_out = mla_attention(q_nope.transpose(0, 2, 1, 3), q_rope.transpose(0, 2, 1, 3),
                            k_nope.transpose(0, 2, 1, 3), k_rope, v.transpose(0, 2, 1, 3))
    mla_out = rms_norm(from_heads(mla_out), mla_out_norm)

    mixed = jnp.concatenate([sb_out, mla_out], axis=-1)
    h = h + mixed @ w_out

    h = h + 0.5 * swiglu(rms_norm(h, ffn2_norm), ffn2_w_gate, ffn2_w_up, ffn2_w_down)

    gate = jax.nn.sigmoid(rms_norm(h, ple_norm) @ w_ple_gate)
    h = h + gate * (p_i @ w_ple_proj)
    return h


def setup_inputs(seed: int = 0) -> dict:
    key = jax.random.key(seed)
    ks = iter(jax.random.split(key, 32))

    def w(shape, fan_in):
        return jax.random.normal(next(ks), shape, jnp.float32) * (fan_in ** -0.5)

    def gain(shape):
        return 1.0 + 0.02 * jax.random.normal(next(ks), shape, jnp.float32)

    L = DEPTH
    return {
        "x": jax.random.normal(next(ks), (BATCH, SEQ, D_MODEL), jnp.float32),
        "p": jax.random.normal(next(ks), (DEPTH, BATCH, SEQ, D_PLE), jnp.float32),
        "ffn1_norm": gain((L, D_MODEL)),
        "ffn1_w_gate": w((L, D_MODEL, D_FF), D_MODEL),
        "ffn1_w_up": w((L, D_MODEL, D_FF), D_MODEL),
        "ffn1_w_down": w((L, D_FF, D_MODEL), D_FF),
        "mix_norm": gain((L, D_MODEL)),
        "w_in": w((L, D_MODEL, IN_COLS), D_MODEL),
        "q_lat_norm": gain((L, Q_LORA)),
        "w_uq": w((L, Q_LORA, MLA_HEADS * MLA_QK_DIM), Q_LORA),
        "kv_lat_norm": gain((L, KV_LORA)),
        "w_ukv": w((L, KV_LORA, MLA_HEADS * (MLA_NOPE_DIM + MLA_V_DIM)), KV_LORA),
        "sb_out_norm": gain((L, SB_WIDTH)),
        "mla_out_norm": gain((L, MLA_WIDTH)),
        "w_out": w((L, D_MIX, D_MODEL), D_MIX),
        "ffn2_norm": gain((L, D_MODEL)),
        "ffn2_w_gate": w((L, D_MODEL, D_FF), D_MODEL),
        "ffn2_w_up": w((L, D_MODEL, D_FF), D_MODEL),
        "ffn2_w_down": w((L, D_FF, D_MODEL), D_FF),
        "ple_norm": gain((L, D_MODEL)),
        "w_ple_gate": w((L, D_MODEL, D_MODEL), D_MODEL),
        "w_ple_proj": w((L, D_PLE, D_MODEL), D_PLE),
        "final_norm": gain((D_MODEL,)),
    }


def reference(x, p, ffn1_norm, ffn1_w_gate, ffn1_w_up, ffn1_w_down,
              mix_norm, w_in, q_lat_norm, w_uq, kv_lat_norm, w_ukv,
              sb_out_norm, mla_out_norm, w_out,
              ffn2_norm, ffn2_w_gate, ffn2_w_up, ffn2_w_down,
              ple_norm, w_ple_gate, w_ple_proj, final_norm):
    cos, sin = rope_cos_sin(x.shape[1])
    h = x
    for i in range(DEPTH):
        h = hybrid_layer(h, p[i], cos, sin,
                         ffn1_norm[i], ffn1_w_gate[i], ffn1_w_up[i], ffn1_w_down[i],
                         mix_norm[i], w_in[i], q_lat_norm[i], w_uq[i], kv_lat_norm[i], w_ukv[i],
                         sb_out_norm[i], mla_out_norm[i], w_out[i],
                         ffn2_norm[i], ffn2_w_gate[i], ffn2_w_up[i], ffn2_w_down[i],
                         ple_norm[i], w_ple_gate[i], w_ple_proj[i])
    return rms_norm(h, final_norm)
```

```python
import numpy as np
import concourse.bass as bass
import concourse.mybir as mybir
from concourse.bass_utils import run_bass_kernel_spmd

F32 = mybir.dt.float32
BF16 = mybir.dt.bfloat16
AF = mybir.ActivationFunctionType
ALU = mybir.AluOpType

D = 1024
T = 2048
TG = 512
NTG = 4
DFF = 2816
NF = 22
NFH = 11
EPS = 1e-6
NEG = -240000.0
SB_SCALE = 0.125
MLA_SCALE = 192.0 ** -0.5
NGL = 43
G_FFN1, G_MIX, G_QLAT, G_KVLAT, G_SB, G_MLA, G_FFN2, G_PLE = 0, 8, 16, 18, 19, 23, 27, 35


class Buf:
    __slots__ = ("w", "r")

    def __init__(self, init=None):
        self.w = dict(init.w) if init is not None else {}
        self.r = dict(init.r) if init is not None else {}
        if init is not None:
            for k, t in init.r.items():
                if k not in self.w or self.w[k][2] < t[2]:
                    self.w[k] = t


def _merge(d, tok):
    k = tok[0]
    if k not in d or d[k][2] < tok[2]:
        d[k] = tok


class Slot:
    def __init__(self, em, name, step=16):
        self.sem = em.nc.alloc_semaphore(name)
        self.key = name
        self.count = 0
        self.step = step
        em.slots.append(self)


class Emitter:
    def __init__(self, nc):
        self.nc = nc
        self.E = dict(pe=nc.tensor, act=nc.scalar, dve=nc.vector, pool=nc.gpsimd, sp=nc.sync)
        self.csem = {e: nc.alloc_semaphore("c_" + e) for e in ("pe", "act", "dve", "pool")}
        self.cnt = dict.fromkeys(self.csem, 0)
        self.waited = {}
        self.ninst = 0
        self.slots = []

    def _wait(self, eng, deps):
        need = {}
        for tok, raw in deps:
            key, sem, val, owner = tok
            if owner == eng and (eng == "pe" or not raw):
                continue
            if self.waited.get((eng, key), 0) >= val:
                continue
            if key not in need or need[key][1] < val:
                need[key] = (sem, val)
        for key, (sem, val) in need.items():
            self.E[eng].wait_ge(sem, val)
            self.waited[(eng, key)] = val
            self.ninst += 1

    def _deps(self, reads, writes):
        deps = []
        for b in reads:
            deps += [(t, True) for t in b.w.values()]
        for b in writes:
            deps += [(t, False) for t in b.w.values()]
            deps += [(t, False) for t in b.r.values()]
        return deps

    def _mark(self, tok, reads, writes):
        for b in writes:
            _merge(b.w, tok)
        for b in reads:
            _merge(b.r, tok)

    def op(self, eng, fn, reads=(), writes=()):
        self._wait(eng, self._deps(reads, writes))
        inst = fn(self.E[eng])
        self.cnt[eng] += 1
        inst.then_inc(self.csem[eng], 1)
        self.ninst += 1
        tok = (eng, self.csem[eng], self.cnt[eng], eng)
        self._mark(tok, reads, writes)
        return tok

    def dma(self, queue, slot, out, in_, reads=(), writes=()):
        self._wait(queue, self._deps(reads, writes))
        inst = self.E[queue].dma_start(out=out, in_=in_)
        slot.count += 16
        inst.then_inc(slot.sem, 16)
        self.ninst += 1
        tok = (slot.key, slot.sem, slot.count, None)
        self._mark(tok, reads, writes)
        return tok

    def collective(self, slot, src, dst, groups, reads=(), writes=()):
        self._wait("pool", self._deps(reads, writes))
        inst = self.nc.gpsimd.collective_compute(
            "AllGather", ALU.bypass, replica_groups=groups, ins=[src.ap().opt()], outs=[dst.ap().opt()]
        )
        slot.count += 1
        inst.then_inc(slot.sem, 1)
        self.ninst += 1
        tok = (slot.key, slot.sem, slot.count, None)
        self._mark(tok, reads, writes)
        return tok

    def fence_buf(self, bufs):
        nb = Buf()
        for b in bufs:
            for t in list(b.w.values()) + list(b.r.values()):
                _merge(nb.w, t)
        return nb

    def fence_all(self):
        nb = Buf()
        for e in self.csem:
            if self.cnt[e] > 0:
                nb.w[e] = (e, self.csem[e], self.cnt[e], e)
        for sl in self.slots:
            if sl.count > 0:
                nb.w[sl.key] = (sl.key, sl.sem, sl.count, None)
        return nb

    def final_wait(self, eng, bufs):
        deps = []
        for b in bufs:
            deps += [(t, True) for t in b.w.values()] + [(t, True) for t in b.r.values()]
        self._wait(eng, deps)


def near_far_steps(J):
    steps = []
    for m in (3, 2, 1, 0):
        for r in (3, 2, 1, 0):
            steps.append((16 * J + 4 * m + r, 128 * m, r))
    for kb in range(16 * J - 1, -1, -1):
        steps.append((kb, 0, None))
    return steps


GROWS = {"ffn": 4224, "mix": 1728, "ple": 640}
MIXSTOP = 99
DEBUG = False
TRACE = False
LAST = {}
PHASES = {"ffn1", "mix", "ffn2", "ple", "sb", "mla", "mixout"}


def build_program(L, final):
    nc = bass.Bass("TRN2", target_bir_lowering=False)
    em = Emitter(nc)

    def din(name, shape):
        return nc.dram_tensor(name, list(shape), F32, kind="ExternalInput")

    xT_d = din("xT", [128, 8, T])
    pT_d = din("pT", [L, 128, 2, T])
    gains_d = din("gains", [128, NGL * L + 8])
    cmat_d = din("cmat", [128, 12, 128])
    rope_d = din("rope", [64, 2, T])
    wsh_d = {"ffn": din("wsh_ffn", [L * 2, GROWS["ffn"] // 8, 2048]),
             "mix": din("wsh_mix", [L, GROWS["mix"] // 8, 2048]),
             "ple": din("wsh_ple", [L, GROWS["ple"] // 8, 2048])}
    out_d = nc.dram_tensor("outT", [128, 8, T], F32, kind="ExternalOutput")
    dbg_d = nc.dram_tensor("dbg", [128, 8, T], F32, kind="ExternalOutput") if DEBUG else None

    ksrc = [nc.dram_tensor(f"ksrc{i}", [128, T], BF16) for i in range(4)]
    kall = [nc.dram_tensor(f"kall{i}", [4 * 128, T], BF16) for i in range(4)]
    vsrc = [nc.dram_tensor(f"vsrc{i}", [128, T], BF16) for i in range(4)]
    vall = [nc.dram_tensor(f"vall{i}", [4 * 128, T], BF16) for i in range(4)]
    csrc = nc.dram_tensor("csrc", [128, T], BF16)
    call = nc.dram_tensor("call", [4 * 128, T], BF16)
    rsrc = nc.dram_tensor("rsrc", [64, T], BF16)
    rall = nc.dram_tensor("rall", [4 * 64, T], BF16)
    ksrc_b, kall_b, vsrc_b, vall_b = ([Buf() for _ in range(4)] for _ in range(4))
    csrc_b, call_b, rsrc_b, rall_b = Buf(), Buf(), Buf(), Buf()
    dbg_b = Buf()
    GROUPS = [[0, 1, 2, 3], [4, 5, 6, 7]]
    ALL8 = [[0, 1, 2, 3, 4, 5, 6, 7]]
    wgath = {}

    def gather_weights(kind, idx):
        rows = GROWS[kind]
        bounce = nc.dram_tensor(f"wb_{kind}{idx}", [rows // 8, 2048], F32)
        full = nc.dram_tensor(f"wf_{kind}{idx}", [rows, 2048], F32)
        bb, fb_ = Buf(), Buf()
        em.dma("sp", Slot(em, f"wb_{kind}{idx}"), bounce.ap(), wsh_d[kind][idx, :, :], writes=[bb])
        em.collective(Slot(em, f"wg_{kind}{idx}", 1), bounce, full, ALL8, reads=[bb], writes=[fb_])
        wgath[(kind, idx)] = (full.ap().rearrange("r c -> (r c)"), fb_)

    def wsrc(kind, idx, off, p, n):
        flat, b = wgath[(kind, idx)]
        return flat[off:off + p * n].rearrange("(p n) -> p n", p=p), b

    hT = nc.alloc_sbuf_tensor("hT", [128, 8, T], F32)
    uT = nc.alloc_sbuf_tensor("uT", [128, 8, T], BF16)
    REGB = 68 * 1024
    reg32 = nc.alloc_sbuf_tensor("reg", [128, REGB // 4], F32)
    reg16 = reg32.bitcast(BF16)
    wslot_t = nc.alloc_sbuf_tensor("wslots", [128, 4, 2048], BF16)
    gains = nc.alloc_sbuf_tensor("gains_sb", [128, NGL * L + 8], F32)
    cm16 = nc.alloc_sbuf_tensor("cm16", [128, 12, 128], BF16)
    ones32 = nc.alloc_sbuf_tensor("ones32", [128, 128], F32)
    wmla = nc.alloc_sbuf_tensor("wmla", [128, 8, 128], BF16)
    rstd_t = [nc.alloc_sbuf_tensor(f"rstd{i}", [128, TG], F32) for i in range(2)]
    tmpA = [nc.alloc_sbuf_tensor(f"tmpA{i}", [128, TG], F32) for i in range(2)]
    tmpB = [nc.alloc_sbuf_tensor(f"tmpB{i}", [128, TG], F32) for i in range(2)]
    l16 = [nc.alloc_sbuf_tensor(f"l16_{i}", [128, TG], BF16) for i in range(2)]
    a16 = [[nc.alloc_sbuf_tensor(f"a16_{i}{j}", [128, TG], BF16) for j in range(2)] for i in range(2)]
    lacc = [nc.alloc_sbuf_tensor(f"lacc{i}", [128, TG], BF16) for i in range(2)]
    PS = [nc.alloc_psum_tensor(f"ps{i}", [128, TG], F32) for i in range(8)]
    PSb = [Buf() for _ in range(8)]
    PS16 = [p.bitcast(BF16) for p in PS]

    hT_b = [Buf() for _ in range(NTG)]
    uT_b = [Buf() for _ in range(NTG)]
    wslot_b = [Buf() for _ in range(4)]
    wslot_s = [Slot(em, f"w{i}") for i in range(4)]
    gains_b, cm_b, ones32_b, sq_b, wmla_b = Buf(), Buf(), Buf(), Buf(), Buf()
    rstd_b = [Buf(), Buf()]
    tmpA_b = [Buf(), Buf()]
    tmpB_b = [Buf(), Buf()]
    l16_b = [Buf(), Buf()]
    a16_b = [[Buf(), Buf()], [Buf(), Buf()]]
    lacc_b = [Buf(), Buf()]
    misc_s = Slot(em, "misc")
    cnt = {"w": 0, "ps": 0, "n": 0}

    ONES, IDENT, TRI, NEG8, MB_SB, MB_MLA = 0, 1, 2, 3, 4, 8

    def r16(off_b, n):
        return reg16[:, off_b // 2: off_b // 2 + n]

    def r32(off_b, n):
        return reg32[:, off_b // 4: off_b // 4 + n]

    O_SQ = 65536
    sq16 = r16(O_SQ, 4 * TG).rearrange("p (a t) -> p a t", a=4)

    def load_w(srcs, ncols_list):
        i = cnt["w"] % 4
        cnt["w"] += 1
        off = 0
        for (src, sb), n in zip(srcs, ncols_list):
            em.dma("pool", wslot_s[i], wslot_t[:, i, off:off + n], src, reads=[sb], writes=[wslot_b[i]])
            off += n
        return i

    def wv(i, off, n, a=None):
        v = wslot_t[:, i, off:off + n]
        if a is not None:
            v = v.rearrange("p (a b) -> p a b", a=a)
        return v

    em.dma("sp", misc_s, gains[:], gains_d[:, :], writes=[gains_b])
    em.dma("pool", misc_s, cm16[:], cmat_d[:, :, :], writes=[cm_b])
    em.op("dve", lambda e: e.memset(ones32[:], 1.0), writes=[ones32_b])
    x_s = Slot(em, "xld")
    for tg in range(NTG):
        em.dma("sp", x_s, hT[:, :, tg * TG:(tg + 1) * TG], xT_d[:, :, tg * TG:(tg + 1) * TG], writes=[hT_b[tg]])

    def norm_tg(src3, nch, Dn, gcol, dst_fn, src_bufs, dst_bufs, rows=128, f32out=False):
        k = cnt["n"] % 2
        cnt["n"] += 1
        pb = 6 + k
        for c0 in range(0, nch, 4):
            c1 = min(nch, c0 + 4)
            em.op("pool", lambda e: e.tensor_tensor(out=sq16[0:rows, 0:c1 - c0, :], in0=src3[:, c0:c1, :],
                                                    in1=src3[:, c0:c1, :], op=ALU.mult),
                  reads=src_bufs, writes=[sq_b])
            for kc in range(c0, c1):
                em.op("pe", lambda e, kc=kc: e.matmul(PS[pb][:, :], lhsT=cm16[0:rows, ONES, :], rhs=sq16[0:rows, kc - c0, :],
                                                      start=(kc == 0), stop=(kc == nch - 1)),
                      reads=[sq_b, cm_b], writes=[PSb[pb]])
        em.op("act", lambda e: e.activation(out=tmpB[k][:], in_=PS[pb][:, :], func=AF.Ln, scale=1.0 / Dn, bias=EPS),
              reads=[PSb[pb]], writes=[tmpB_b[k]])
        em.op("act", lambda e: e.activation(out=rstd_t[k][:], in_=tmpB[k][:], func=AF.Exp, scale=-0.5),
              reads=[tmpB_b[k]], writes=[rstd_b[k]])
        for kc in range(nch):
            em.op("dve", lambda e, kc=kc: e.scalar_tensor_tensor(
                out=dst_fn(kc), in0=src3[:, kc, :], scalar=gains[0:rows, gcol + kc:gcol + kc + 1],
                in1=rstd_t[k][0:rows, :], op0=ALU.mult, op1=ALU.mult),
                reads=list(src_bufs) + [rstd_b[k], gains_b], writes=dst_bufs)

    def norm_h_to_u(gcol):
        for tg in range(NTG):
            sl = slice(tg * TG, (tg + 1) * TG)
            norm_tg(hT[:, :, sl], 8, D, gcol, lambda kc, sl=sl: uT[:, kc, sl], [hT_b[tg]], [uT_b[tg]])

    evac_rr = {"i": 0}

    def evac(out_ap, ps_i, reads_extra=(), writes=(), rows=slice(0, 128), cols=slice(0, TG)):
        evac_rr["i"] += 1
        if evac_rr["i"] % 2:
            em.op("act", lambda e: e.activation(out=out_ap, in_=PS[ps_i][rows, cols], func=AF.Copy),
                  reads=[PSb[ps_i]] + list(reads_extra), writes=writes)
        else:
            em.op("dve", lambda e: e.tensor_copy(out=out_ap, in_=PS[ps_i][rows, cols]),
                  reads=[PSb[ps_i]] + list(reads_extra), writes=writes)

    def ffn(l, which, gcol):
        norm_h_to_u(gcol)
        actT = r16(0, NFH * T).rearrange("p (f t) -> p f t", f=NFH)
        fb = em.fence_all()
        act_b = [Buf(fb) for _ in range(NTG)]
        gi = l * 2 + which
        it = 0
        for half in range(2):
            for fi in range(NFH):
                f = half * NFH + fi
                ws = load_w([wsrc("ffn", gi, f * 128 * 2048, 128, 2048)], [2048])
                W = wv(ws, 0, 2048, a=16)
                for tg in range(NTG):
                    sl = slice(tg * TG, (tg + 1) * TG)
                    pg, pu = (0, 1) if it % 2 == 0 else (2, 3)
                    k = it % 2
                    it += 1
                    for gu, pb in ((0, pg), (1, pu)):
                        for kc in range(8):
                            em.op("pe", lambda e, gu=gu, pb=pb, kc=kc: e.matmul(
                                PS[pb][:, :], lhsT=W[:, gu * 8 + kc, :], rhs=uT[:, kc, sl],
                                start=(kc == 0), stop=(kc == 7)),
                                reads=[wslot_b[ws], uT_b[tg]], writes=[PSb[pb]])
                    em.op("act", lambda e: e.activation(out=tmpA[k][:], in_=PS[pg][:, :], func=AF.Silu),
                          reads=[PSb[pg]], writes=[tmpA_b[k]])
                    em.op("dve", lambda e: e.tensor_tensor(out=actT[:, fi, sl], in0=tmpA[k][:], in1=PS[pu][:, :],
                                                           op=ALU.mult),
                          reads=[tmpA_b[k], PSb[pu]], writes=[act_b[tg]])
            for dc in range(8):
                wsa, wsb_ = wsrc("ffn", gi, NF * 128 * 2048 + dc * 128 * NF * 128, 128, NF * 128)
                ws = load_w([(wsa[:, half * NFH * 128:(half + 1) * NFH * 128], wsb_)], [NFH * 128])
                W = wv(ws, 0, NFH * 128, a=NFH)
                for tg in range(NTG):
                    sl = slice(tg * TG, (tg + 1) * TG)
                    pb = 4 + (it % 2)
                    it += 1
                    for fi in range(NFH):
                        em.op("pe", lambda e, fi=fi: e.matmul(PS[pb][:, :], lhsT=W[:, fi, :], rhs=actT[:, fi, sl],
                                                              start=(fi == 0), stop=(fi == NFH - 1)),
                              reads=[wslot_b[ws], act_b[tg]], writes=[PSb[pb]])
                    em.op("dve", lambda e: e.scalar_tensor_tensor(
                        out=hT[:, dc, sl], in0=PS[pb][:, :], scalar=0.5, in1=hT[:, dc, sl],
                        op0=ALU.mult, op1=ALU.add),
                        reads=[PSb[pb], hT_b[tg]], writes=[hT_b[tg]])

    O_QSB, O_CQN, O_ROPE, O_KT, O_V = 0, 16384, 24576, 32768, 49152
    O_QP, O_QR, O_CALL, O_CTOK = 0, 32768, 49152, 16384

    def mix(l):
        gb = l * NGL
        norm_h_to_u(gb + G_MIX)
        fb = em.fence_all()
        QsbT = r16(O_QSB, 4 * T).rearrange("p (a t) -> p a t", a=4)
        cqnT = r16(O_CQN, 2 * T).rearrange("p (a t) -> p a t", a=2)
        ropeT = r16(O_ROPE, 2 * T).rearrange("p (a t) -> p a t", a=2)
        qsb_b, cqn_b, rope_b = Buf(fb), Buf(fb), Buf(fb)
        kt_b, v_b = Buf(fb), Buf(fb)
        kst = [r16(O_KT + i * 4096, T) for i in range(2)]
        kst_b = [Buf(fb), Buf(fb)]
        kst_s = [Slot(em, f"kst{i}_{l}") for i in range(2)]
        vst = [r16(O_KT + 8192 + i * 1024, 512) for i in range(2)]
        vst_b = [Buf(fb), Buf(fb)]
        vst_s = [Slot(em, f"vst{i}_{l}") for i in range(2)]
        c32 = r32(O_V, 2 * TG).rearrange("p (a t) -> p a t", a=2)
        c32_b = Buf(fb)
        ckvn = r16(O_V + 4096, T)
        ckvn_b = Buf(fb)
        kr16 = r16(O_V + 8192, T)
        kr16_b = Buf(fb)
        t1, t2 = tmpB[0], tmpB[1]
        t1_b, t2_b = tmpB_b[0], tmpB_b[1]
        st_s = Slot(em, f"st_{l}")
        rope_s = Slot(em, f"rope_{l}")
        if "norope" not in PHASES:
            em.dma("pool", rope_s, ropeT[0:64, :, :], rope_d[:, :, :], writes=[rope_b])

        def proj_tile(ws, wcols, M, ps_i, tg, prow=0):
            sl = slice(tg * TG, (tg + 1) * TG)
            W = wv(ws, 0, 1024, a=8)
            for kc in range(8):
                em.op("pe", lambda e, kc=kc: e.matmul(PS[ps_i][prow:prow + M, :], lhsT=W[:, kc, wcols], rhs=uT[:, kc, sl],
                                                      start=(kc == 0), stop=(kc == 7)),
                      reads=[wslot_b[ws], uT_b[tg]], writes=[PSb[ps_i]])

        pr = {"i": 0}

        def next_ps():
            pr["i"] += 1
            return pr["i"] % 6

        agk_s, agv_s = Slot(em, f"agk{l}", 1), Slot(em, f"agv{l}", 1)
        for hp in (range(4) if "nosbq" not in PHASES else ()):
            ws = load_w([wsrc("mix", l, (hp) * 131072, 128, 1024)], [1024])
            for tg in range(NTG):
                pb = next_ps()
                proj_tile(ws, slice(0, 128), 128, pb, tg)
                evac(QsbT[:, hp, tg * TG:(tg + 1) * TG], pb, writes=[qsb_b])
        if MIXSTOP <= 1:
            return
        for hp in range(4):
            ws = load_w([wsrc("mix", l, (4 + hp) * 131072, 128, 1024)], [1024])
            k = hp % 2
            for tg in range(NTG):
                pb = next_ps()
                proj_tile(ws, slice(0, 128), 128, pb, tg)
                evac(kst[k][:, tg * TG:(tg + 1) * TG], pb, writes=[kst_b[k]])
            em.dma("sp", kst_s[k], ksrc[hp].ap(), kst[k], reads=[kst_b[k]], writes=[ksrc_b[hp]])
            em.collective(agk_s, ksrc[hp], kall[hp], GROUPS, reads=[ksrc_b[hp]], writes=[kall_b[hp]])
        if MIXSTOP <= 2:
            return
        wsv = [load_w([wsrc("mix", l, 1572864 + i * 262144, 128, 2048)], [2048]) for i in range(2)]
        vi = 0
        for hp in range(4):
            for tbg in range(4):
                pb = next_ps()
                for q in range(4):
                    tb = 4 * tbg + q
                    for kc in range(8):
                        Wv = wv(wsv[kc // 4], 0, 2048, a=4)
                        em.op("pe", lambda e, kc=kc, Wv=Wv, q=q, tb=tb: e.matmul(
                            PS[pb][:, q * 128:(q + 1) * 128], lhsT=uT[:, kc, tb * 128:(tb + 1) * 128],
                            rhs=Wv[:, kc % 4, hp * 128:(hp + 1) * 128], start=(kc == 0), stop=(kc == 7)),
                            reads=[wslot_b[wsv[kc // 4]], uT_b[tb // 4]], writes=[PSb[pb]])
                k = vi % 2
                vi += 1
                evac(vst[k], pb, writes=[vst_b[k]])
                em.dma("sp", vst_s[k], vsrc[hp][:, tbg * 512:(tbg + 1) * 512], vst[k], reads=[vst_b[k]], writes=[vsrc_b[hp]])
            em.collective(agv_s, vsrc[hp], vall[hp], GROUPS, reads=[vsrc_b[hp]], writes=[vall_b[hp]])
        if MIXSTOP <= 3:
            return
        ws = load_w([wsrc("mix", l, 10 * 131072, 128, 1024)], [1024])
        for tg in range(NTG):
            sl = slice(tg * TG, (tg + 1) * TG)
            pb = next_ps()
            proj_tile(ws, slice(0, 128), 128, pb, tg)
            evac(c32[:, 0, :], pb, writes=[c32_b])
            norm_tg(c32[:, 0:1, :], 1, 128, gb + G_KVLAT, lambda kc, sl=sl: ckvn[:, sl], [c32_b], [ckvn_b])
        em.dma("sp", st_s, csrc[:, :], ckvn, reads=[ckvn_b], writes=[csrc_b])
        em.collective(Slot(em, f"agc{l}", 1), csrc, call, GROUPS, reads=[csrc_b], writes=[call_b])
        if MIXSTOP <= 4:
            return
        ws = load_w([wsrc("mix", l, 11 * 131072, 128, 1024)], [1024])
        for tg in range(NTG):
            sl = slice(tg * TG, (tg + 1) * TG)
            pa, pbk = next_ps(), next_ps()
            proj_tile(ws, slice(0, 64), 64, pa, tg)
            proj_tile(ws, slice(64, 128), 64, pbk, tg)
            em.op("dve", lambda e: e.tensor_tensor(out=t1[0:64, :], in0=PS[pa][0:64, :], in1=ropeT[0:64, 0, sl], op=ALU.mult),
                  reads=[PSb[pa], rope_b], writes=[t1_b])
            em.op("dve", lambda e: e.tensor_tensor(out=t2[0:64, :], in0=PS[pbk][0:64, :], in1=ropeT[0:64, 1, sl], op=ALU.mult),
                  reads=[PSb[pbk], rope_b], writes=[t2_b])
            em.op("pool", lambda e: e.tensor_tensor(out=kr16[0:64, sl], in0=t1[0:64, :], in1=t2[0:64, :], op=ALU.add),
                  reads=[t1_b, t2_b], writes=[kr16_b])
        em.dma("sp", st_s, rsrc[:, :], kr16[0:64, :], reads=[kr16_b], writes=[rsrc_b])
        em.collective(Slot(em, f"agr{l}", 1), rsrc, rall, GROUPS, reads=[rsrc_b], writes=[rall_b])
        if MIXSTOP <= 5:
            return
        wsq = [load_w([wsrc("mix", l, (8 + i) * 131072, 128, 1024)], [1024]) for i in range(2)]
        for tg in range(NTG):
            sl = slice(tg * TG, (tg + 1) * TG)
            for i in range(2):
                pb = next_ps()
                proj_tile(wsq[i], slice(0, 128), 128, pb, tg)
                evac(c32[:, i, :], pb, writes=[c32_b])
            norm_tg(c32[:, :, :], 2, 256, gb + G_QLAT, lambda kc, sl=sl: cqnT[:, kc, sl], [c32_b], [cqn_b])

        if MIXSTOP <= 6:
            return
        KT = r16(O_KT, 4 * T).rearrange("p (r t) -> p r t", r=4)
        V = r16(O_V, 64 * 128).rearrange("p (k d) -> p k d", k=64)
        ktf = em.fence_buf(kst_b + vst_b)
        vf = em.fence_buf([c32_b, ckvn_b, kr16_b])
        kt_b, v_b = Buf(ktf), Buf(vf)
        kt_s, v_s = Slot(em, f"kt_{l}"), Slot(em, f"v_{l}")
        sbT = uT[:, 0:4, :]
        mlaT = uT[:, 4:8, :]
        attn_b = [Buf(em.fence_buf(uT_b)) for _ in range(NTG)]
        ZA, ZB, OA, OB = (0, 1), (2, 3), 4, 5

        for hp in (range(4) if "sb" in PHASES else ()):
            for r in range(4):
                em.dma("sp", kt_s, KT[:, r, :], kall[hp][r * 128:(r + 1) * 128, :],
                       reads=[kall_b[hp]], writes=[kt_b])
                em.dma("sp", v_s, V[:, r * 16:(r + 1) * 16, :],
                       vall[hp][r * 128:(r + 1) * 128, :].rearrange("p (j d) -> p j d", j=16),
                       reads=[vall_b[hp]], writes=[v_b])
            for J in range(NTG):
                steps = near_far_steps(J)
                n = len(steps)
                chains = [dict(rows=slice(0, 64), Z=ZA, O=OA, k=0), dict(rows=slice(64, 128), Z=ZB, O=OB, k=1)]
                q0 = J * TG

                def qk(ch, s):
                    kb, c0, mr = steps[s]
                    r, j = kb % 4, kb // 4
                    zb = ch["Z"][s % 2]
                    rows = ch["rows"]
                    em.op("pe", lambda e: e.matmul(PS[zb][:, c0:TG], lhsT=KT[rows, r, j * 128:(j + 1) * 128],
                                                   rhs=QsbT[rows, hp, q0 + c0:q0 + TG], start=True, stop=False),
                          reads=[kt_b, qsb_b], writes=[PSb[zb]])
                    if mr is not None:
                        em.op("pe", lambda e: e.matmul(PS[zb][:, c0:c0 + 128], lhsT=cm16[:, IDENT, :],
                                                       rhs=cm16[:, MB_SB + mr, :], start=False, stop=False),
                              reads=[cm_b], writes=[PSb[zb]])

                def ex(ch, s):
                    kb, c0, mr = steps[s]
                    zb = ch["Z"][s % 2]
                    k = ch["k"]
                    em.op("act", lambda e: e.activation(out=tmpA[k][:, c0:TG], in_=PS[zb][:, c0:TG], func=AF.Exp, scale=SB_SCALE),
                          reads=[PSb[zb]], writes=[tmpA_b[k]])

                def ln(ch, s):
                    kb, c0, mr = steps[s]
                    k = ch["k"]
                    em.op("act", lambda e: e.activation(out=l16[k][:, c0:TG], in_=tmpA[k][:, c0:TG], func=AF.Ln, scale=1.0, bias=1.0),
                          reads=[tmpA_b[k]], writes=[l16_b[k]])

                def tri(ch, s):
                    kb, c0, mr = steps[s]
                    zb = ch["Z"][s % 2]
                    k = ch["k"]
                    em.op("pe", lambda e: e.matmul(PS[zb][:, c0:TG], lhsT=cm16[:, TRI, :], rhs=l16[k][:, c0:TG],
                                                   start=False, stop=(s == 0)),
                          reads=[cm_b, l16_b[k]], writes=[PSb[zb]])
                    if s > 0:
                        em.op("pe", lambda e: e.matmul(PS[zb][:, c0:TG], lhsT=cm16[:, NEG8, :], rhs=lacc[k][:, c0:TG],
                                                       start=False, stop=True),
                              reads=[cm_b, lacc_b[k]], writes=[PSb[zb]])
                    if s < n - 1:
                        em.op("pool", lambda e: e.tensor_tensor(out=lacc[k][:, c0:TG], in0=lacc[k][:, c0:TG],
                                                                in1=l16[k][:, c0:TG], op=ALU.add),
                              reads=[l16_b[k], lacc_b[k]], writes=[lacc_b[k]])

                def ex2(ch, s):
                    kb, c0, mr = steps[s]
                    zb = ch["Z"][s % 2]
                    k = ch["k"]
                    em.op("act", lambda e: e.activation(out=a16[k][s % 2][:, c0:TG], in_=PS[zb][:, c0:TG], func=AF.Exp, scale=SB_SCALE),
                          reads=[PSb[zb]], writes=[a16_b[k][s % 2]])

                def av(ch, s):
                    kb, c0, mr = steps[s]
                    k = ch["k"]
                    rows = ch["rows"]
                    ob = ch["O"]
                    em.op("pe", lambda e: e.matmul(PS[ob][rows, c0:TG], lhsT=V[:, (kb % 4) * 16 + kb // 4, rows],
                                                   rhs=a16[k][s % 2][:, c0:TG], start=(s == 0), stop=(s == n - 1)),
                          reads=[v_b, a16_b[k][s % 2]], writes=[PSb[ob]])

                for ch in chains:
                    em.op("pool", lambda e, ch=ch: e.memset(lacc[ch["k"]][:], 0.0), writes=[lacc_b[ch["k"]]])
                for ch in chains:
                    qk(ch, 0)
                for ch in chains:
                    ex(ch, 0)
                for s in range(n):
                    if s + 1 < n:
                        for ch in chains:
                            qk(ch, s + 1)
                    if s >= 1:
                        for ch in chains:
                            av(ch, s - 1)
                    for ch in chains:
                        ln(ch, s)
                    for ch in chains:
                        tri(ch, s)
                    if s + 1 < n:
                        for ch in chains:
                            ex(ch, s + 1)
                    for ch in chains:
                        ex2(ch, s)
                for ch in chains:
                    av(ch, n - 1)
                for ch in chains:
                    rows, ob = ch["rows"], ch["O"]
                    em.op("dve", lambda e, rows=rows, ob=ob: e.tensor_copy(out=sbT[rows, hp, q0:q0 + TG], in_=PS[ob][rows, :]),
                          reads=[PSb[ob]], writes=[attn_b[J]])

        if MIXSTOP <= 7:
            return
        fq = em.fence_buf([qsb_b])
        QpT = r16(O_QP, 4 * T).rearrange("p (a t) -> p a t", a=4)
        QrT = r16(O_QR, 4 * T).rearrange("p (a t) -> p a t", a=4)
        qp_b = Buf(fq)
        qr_b = Buf(em.fence_buf([kt_b]))
        callT = r16(O_CALL, 4 * T).rearrange("p (r t) -> p r t", r=4)
        call_sb = Buf(em.fence_buf([v_b]))
        call_s = Slot(em, f"call_{l}")
        for r in range(4):
            em.dma("sp", call_s, callT[:, r, :], call[r * 128:(r + 1) * 128, :], reads=[call_b], writes=[call_sb])
        for h in range(4):
            ws = load_w([wsrc("mix", l, 2097152 + h * 65536, 128, 512)], [512])
            Wq = wv(ws, 0, 512, a=2)
            for tg in range(NTG):
                sl = slice(tg * TG, (tg + 1) * TG)
                k = tg % 2
                pn, pr_, prr = next_ps(), next_ps(), next_ps()
                for (pb, cols, M) in ((pn, slice(0, 128), 128), (pr_, slice(128, 192), 64), (prr, slice(192, 256), 64)):
                    for kc in range(2):
                        em.op("pe", lambda e, pb=pb, cols=cols, M=M, kc=kc: e.matmul(
                            PS[pb][0:M, :], lhsT=Wq[:, kc, cols], rhs=cqnT[:, kc, sl], start=(kc == 0), stop=(kc == 1)),
                            reads=[wslot_b[ws], cqn_b], writes=[PSb[pb]])
                evac(a16[k][0][:, :], pn, writes=[a16_b[k][0]])
                em.op("dve", lambda e: e.tensor_tensor(out=t1[0:64, :], in0=PS[pr_][0:64, :], in1=ropeT[0:64, 0, sl], op=ALU.mult),
                      reads=[PSb[pr_], rope_b], writes=[t1_b])
                em.op("dve", lambda e: e.tensor_tensor(out=t2[0:64, :], in0=PS[prr][0:64, :], in1=ropeT[0:64, 1, sl], op=ALU.mult),
                      reads=[PSb[prr], rope_b], writes=[t2_b])
                em.op("pool", lambda e: e.tensor_tensor(out=QrT[0:64, h, sl], in0=t1[0:64, :], in1=t2[0:64, :], op=ALU.add),
                      reads=[t1_b, t2_b], writes=[qr_b])
                pq = next_ps()
                em.op("pe", lambda e: e.matmul(PS[pq][:, :], lhsT=wmla[:, h, :], rhs=a16[k][0][:, :], start=True, stop=True),
                      reads=[wmla_b, a16_b[k][0]], writes=[PSb[pq]])
                evac(QpT[:, h, sl], pq, writes=[qp_b])
        if MIXSTOP <= 8:
            return
        krT = wslot_t
        kr_s = Slot(em, f"kr_{l}")
        for r in range(4):
            em.dma("sp", kr_s, krT[0:64, r, :], rall[r * 64:(r + 1) * 64, :], reads=[rall_b], writes=wslot_b)
        ctok = r16(O_CTOK, 64 * 128).rearrange("p (k d) -> p k d", k=64)
        ctok_b = Buf(em.fence_buf([cqn_b, rope_b, t1_b, t2_b, qr_b, qp_b]))
        for r in range(4):
            for g in range(4):
                pb = next_ps()
                for q in range(4):
                    j = 4 * g + q
                    em.op("pe", lambda e, q=q, j=j: e.transpose(PS16[pb][:, q * 128:(q + 1) * 128],
                                                              callT[:, r, j * 128:(j + 1) * 128], cm16[:, IDENT, :]),
                          reads=[call_sb, cm_b], writes=[PSb[pb]])
                dst = ctok[:, r * 16 + 4 * g:r * 16 + 4 * g + 4, :]
                src = PS16[pb][:, 0:512].rearrange("p (a b) -> p a b", a=4)
                em.op("dve", lambda e: e.tensor_copy(out=dst, in_=src), reads=[PSb[pb]], writes=[ctok_b])

        gather_weights("ffn", 2 * l + 1)
        gather_weights("ple", l)
        if l + 1 < L:
            gather_weights("ffn", 2 * l + 2)
            gather_weights("mix", l + 1)
        if MIXSTOP <= 9:
            return
        for h in (range(4) if "mla" in PHASES else ()):
            for pair in ((0, 3), (1, 2)):
                chains = []
                for ci, J in enumerate(pair):
                    chains.append(dict(J=J, steps=near_far_steps(J), Z=(0, 1) if ci == 0 else (2, 3), O=4 + ci, k=ci))
                nmax = max(len(ch["steps"]) for ch in chains)

                def s12(ch, s):
                    kb, c0, mr = ch["steps"][s]
                    r, j = kb % 4, kb // 4
                    zb = ch["Z"][s % 2]
                    q0 = ch["J"] * TG
                    em.op("pe", lambda e: e.matmul(PS[zb][:, c0:TG], lhsT=callT[:, r, j * 128:(j + 1) * 128],
                                                   rhs=QpT[:, h, q0 + c0:q0 + TG], start=True, stop=False),
                          reads=[call_sb, qp_b], writes=[PSb[zb]])
                    em.op("pe", lambda e: e.matmul(PS[zb][:, c0:TG], lhsT=krT[0:64, r, j * 128:(j + 1) * 128],
                                                   rhs=QrT[0:64, h, q0 + c0:q0 + TG], start=False, stop=(mr is None)),
                          reads=list(wslot_b) + [qr_b], writes=[PSb[zb]])
                    if mr is not None:
                        em.op("pe", lambda e: e.matmul(PS[zb][:, c0:c0 + 128], lhsT=cm16[:, IDENT, :],
                                                       rhs=cm16[:, MB_MLA + mr, :], start=False, stop=True),
                              reads=[cm_b], writes=[PSb[zb]])

                def pexp(ch, s):
                    kb, c0, mr = ch["steps"][s]
                    zb = ch["Z"][s % 2]
                    k = ch["k"]
                    em.op("act", lambda e: e.activation(out=a16[k][s % 2][:, c0:TG], in_=PS[zb][:, c0:TG], func=AF.Exp, scale=MLA_SCALE),
                          reads=[PSb[zb]], writes=[a16_b[k][s % 2]])

                def avm(ch, s):
                    kb, c0, mr = ch["steps"][s]
                    k = ch["k"]
                    ob = ch["O"]
                    n = len(ch["steps"])
                    em.op("pe", lambda e: e.matmul(PS[ob][:, c0:TG], lhsT=ctok[:, (kb % 4) * 16 + kb // 4, :],
                                                   rhs=a16[k][s % 2][:, c0:TG], start=(s == 0), stop=(s == n - 1)),
                          reads=[ctok_b, a16_b[k][s % 2]], writes=[PSb[ob]])
                    em.op("dve", lambda e: e.tensor_tensor(out=tmpB[k][:, c0:TG], in0=tmpB[k][:, c0:TG],
                                                           in1=a16[k][s % 2][:, c0:TG], op=ALU.add),
                          reads=[a16_b[k][s % 2], tmpB_b[k]], writes=[tmpB_b[k]])

                for ch in chains:
                    em.op("dve", lambda e, ch=ch: e.memset(tmpB[ch["k"]][:], 0.0), writes=[tmpB_b[ch["k"]]])
                for ch in chains:
                    s12(ch, 0)
                for s in range(nmax):
                    for ch in chains:
                        if s + 1 < len(ch["steps"]):
                            s12(ch, s + 1)
                    for ch in chains:
                        if s < len(ch["steps"]):
                            pexp(ch, s)
                    for ch in chains:
                        if s < len(ch["steps"]):
                            avm(ch, s)
                for ch in chains:
                    k, ob, J = ch["k"], ch["O"], ch["J"]
                    q0 = J * TG
                    em.op("pe", lambda e: e.matmul(PS[6][:, :], lhsT=ones32[:, :], rhs=tmpB[k][:, :], start=True, stop=True),
                          reads=[ones32_b, tmpB_b[k]], writes=[PSb[6]])
                    em.op("act", lambda e: e.activation(out=rstd_t[k][:], in_=PS[6][:, :], func=AF.Ln),
                          reads=[PSb[6]], writes=[rstd_b[k]])
                    em.op("act", lambda e: e.activation(out=tmpA[k][:], in_=rstd_t[k][:], func=AF.Exp, scale=-1.0),
                          reads=[rstd_b[k]], writes=[tmpA_b[k]])
                    em.op("dve", lambda e: e.tensor_tensor(out=l16[k][:, :], in0=PS[ob][:, :], in1=tmpA[k][:], op=ALU.mult),
                          reads=[PSb[ob], tmpA_b[k]], writes=[l16_b[k]])
                    em.op("pe", lambda e: e.matmul(PS[7][:, :], lhsT=wmla[:, 4 + h, :], rhs=l16[k][:, :], start=True, stop=True),
                          reads=[wmla_b, l16_b[k]], writes=[PSb[7]])
                    evac(mlaT[:, h, q0:q0 + TG], 7, writes=[attn_b[J]])

        if DEBUG:
            em.dma("pool", Slot(em, f"dbg{l}"), dbg_d[:, :, :], uT[:, :, :], reads=attn_b, writes=[dbg_b])
        if MIXSTOP <= 10:
            return
        for tg in range(NTG):
            sl = slice(tg * TG, (tg + 1) * TG)
            norm_tg(uT[:, 0:4, sl], 4, 512, gb + G_SB, lambda kc, sl=sl: uT[:, kc, sl], [attn_b[tg]], [attn_b[tg]])
            norm_tg(uT[:, 4:8, sl], 4, 512, gb + G_MLA, lambda kc, sl=sl: uT[:, 4 + kc, sl], [attn_b[tg]], [attn_b[tg]])
        for dc in range(8):
            ws = load_w([wsrc("mix", l, 2490368 + dc * 131072, 128, 1024)], [1024])
            W = wv(ws, 0, 1024, a=8)
            for tg in range(NTG):
                sl = slice(tg * TG, (tg + 1) * TG)
                pb = next_ps()
                for kc in range(8):
                    em.op("pe", lambda e, kc=kc: e.matmul(PS[pb][:, :], lhsT=W[:, kc, :], rhs=uT[:, kc, sl],
                                                          start=(kc == 0), stop=(kc == 7)),
                          reads=[wslot_b[ws], attn_b[tg]], writes=[PSb[pb]])
                em.op("dve", lambda e: e.tensor_tensor(out=hT[:, dc, sl], in0=PS[pb][:, :], in1=hT[:, dc, sl], op=ALU.add),
                      reads=[PSb[pb], hT_b[tg]], writes=[hT_b[tg]])
        for tg in range(NTG):
            uT_b[tg] = Buf(em.fence_buf([attn_b[tg], uT_b[tg]]))

    def ple(l):
        gb = l * NGL
        norm_h_to_u(gb + G_PLE)
        fb = em.fence_all()
        pT16 = r16(0, 2 * T).rearrange("p (a t) -> p a t", a=2)
        p_b = Buf(fb)
        p_s = Slot(em, f"p_{l}")
        em.dma("pool", p_s, pT16, pT_d[l, :, :, :], writes=[p_b])
        it = 0
        for dc in range(8):
            ws = load_w([wsrc("ple", l, dc * 131072, 128, 1024), wsrc("ple", l, 1048576 + dc * 32768, 128, 256)],
                        [1024, 256])
            Wg = wv(ws, 0, 1024, a=8)
            Wp = wv(ws, 1024, 256, a=2)
            for tg in range(NTG):
                sl = slice(tg * TG, (tg + 1) * TG)
                pg, pp = (0, 1) if it % 2 == 0 else (2, 3)
                k = it % 2
                it += 1
                for kc in range(8):
                    em.op("pe", lambda e, kc=kc: e.matmul(PS[pg][:, :], lhsT=Wg[:, kc, :], rhs=uT[:, kc, sl],
                                                          start=(kc == 0), stop=(kc == 7)),
                          reads=[wslot_b[ws], uT_b[tg]], writes=[PSb[pg]])
                for kc in range(2):
                    em.op("pe", lambda e, kc=kc: e.matmul(PS[pp][:, :], lhsT=Wp[:, kc, :], rhs=pT16[:, kc, sl],
                                                          start=(kc == 0), stop=(kc == 1)),
                          reads=[wslot_b[ws], p_b], writes=[PSb[pp]])
                em.op("act", lambda e: e.activation(out=tmpA[k][:], in_=PS[pg][:, :], func=AF.Sigmoid),
                      reads=[PSb[pg]], writes=[tmpA_b[k]])
                em.op("dve", lambda e: e.tensor_tensor(out=tmpB[k][:], in0=tmpA[k][:], in1=PS[pp][:, :], op=ALU.mult),
                      reads=[tmpA_b[k], PSb[pp]], writes=[tmpB_b[k]])
                em.op("pool", lambda e: e.tensor_tensor(out=hT[:, dc, sl], in0=hT[:, dc, sl], in1=tmpB[k][:], op=ALU.add),
                      reads=[tmpB_b[k], hT_b[tg]], writes=[hT_b[tg]])

    for l in range(L):
        gb = l * NGL
        if l == 0:
            gather_weights("ffn", 0)
            gather_weights("mix", 0)
        if "ffn1" in PHASES:
            ffn(l, 0, gb + G_FFN1)
        flat, mb = wgath[("mix", l)]
        em.dma("pool", misc_s, wmla[:, 0:4, :], flat[2359296:2359296 + 65536].rearrange("(h p m) -> p h m", h=4, p=128),
               reads=[mb], writes=[wmla_b])
        em.dma("pool", misc_s, wmla[:, 4:8, :], flat[2424832:2424832 + 65536].rearrange("(p h m) -> p h m", p=128, h=4),
               reads=[mb], writes=[wmla_b])
        if "mix" in PHASES:
            mix(l)
        if "ffn2" in PHASES:
            ffn(l, 1, gb + G_FFN2)
        if "ple" in PHASES:
            ple(l)

    out_s = Slot(em, "outst")
    out_b = Buf()
    if DEBUG:
        em.final_wait("sp", [dbg_b])
    if final:
        fb = em.fence_all()
        ost = [r32(i * 16384, 8 * TG).rearrange("p (a t) -> p a t", a=8) for i in range(2)]
        ost_b = [Buf(fb), Buf(fb)]
        for tg in range(NTG):
            sl = slice(tg * TG, (tg + 1) * TG)
            k = tg % 2
            norm_tg(hT[:, :, sl], 8, D, NGL * L, lambda kc, k=k: ost[k][:, kc, :], [hT_b[tg]], [ost_b[k]])
            em.dma("sp", out_s, out_d[:, :, sl], ost[k], reads=[ost_b[k]], writes=[out_b])
    else:
        for tg in range(NTG):
            sl = slice(tg * TG, (tg + 1) * TG)
            em.dma("sp", out_s, out_d[:, :, sl], hT[:, :, sl], reads=[hT_b[tg]], writes=[out_b])
    em.final_wait("sp", [out_b])
    return nc, em


def _tile_lhsT(W):
    K, N = W.shape
    return np.ascontiguousarray(W.reshape(K // 128, 128, N // 128, 128).transpose(2, 1, 0, 3))


def _positions(c):
    return (np.arange(16)[:, None] * 512 + c * 128 + np.arange(128)[None, :]).reshape(-1)


def _prep_weights(inp, layers):
    f32 = np.float32
    ffn_g, mix_g, ple_g = [], [], []
    for l in layers:
        for which in ("ffn1", "ffn2"):
            g = _tile_lhsT(np.asarray(inp[which + "_w_gate"][l], f32))
            u = _tile_lhsT(np.asarray(inp[which + "_w_up"][l], f32))
            wgu = np.stack([g, u], axis=2).reshape(-1)
            wd = _tile_lhsT(np.asarray(inp[which + "_w_down"][l], f32)).reshape(-1)
            ffn_g.append(np.concatenate([wgu, wd]).reshape(GROWS["ffn"], 2048))
        w_in = np.asarray(inp["w_in"][l], f32)
        rot = np.concatenate([w_in[:, 1920:1984], w_in[:, 1952:1984], w_in[:, 1920:1952]], axis=1)
        winA = np.concatenate([_tile_lhsT(w_in[:, 0:512]), _tile_lhsT(w_in[:, 512:1024]), _tile_lhsT(w_in[:, 1536:1792]),
                               _tile_lhsT(w_in[:, 1792:1920]), _tile_lhsT(rot)], axis=0).reshape(-1)
        v = w_in[:, 1024:1536].reshape(8, 128, 512).transpose(1, 0, 2)
        winV = np.stack([v[:, 0:4, :].reshape(128, 2048), v[:, 4:8, :].reshape(128, 2048)], axis=0).reshape(-1)
        w_uq = np.asarray(inp["w_uq"][l], f32)
        wuq = []
        for h in range(4):
            b0 = h * 192
            cols = np.concatenate([w_uq[:, b0:b0 + 128], w_uq[:, b0 + 128:b0 + 192], w_uq[:, b0 + 160:b0 + 192],
                                   w_uq[:, b0 + 128:b0 + 160]], axis=1)
            wuq.append(cols.reshape(2, 128, 256).transpose(1, 0, 2).reshape(-1))
        wuq = np.concatenate(wuq)
        w_ukv = np.asarray(inp["w_ukv"][l], f32)
        wukT = np.concatenate([np.ascontiguousarray(w_ukv[:, h * 256:h * 256 + 128].T).reshape(-1) for h in range(4)])
        wuv = np.ascontiguousarray(w_ukv.reshape(128, 4, 256)[:, :, 128:256]).reshape(-1)
        wout = _tile_lhsT(np.asarray(inp["w_out"][l], f32)).reshape(-1)
        mix_g.append(np.concatenate([winA, winV, wuq, wukT, wuv, wout]).reshape(GROWS["mix"], 2048))
        wpg = _tile_lhsT(np.asarray(inp["w_ple_gate"][l], f32)).reshape(-1)
        wpp = _tile_lhsT(np.asarray(inp["w_ple_proj"][l], f32)).reshape(-1)
        ple_g.append(np.concatenate([wpg, wpp]).reshape(GROWS["ple"], 2048))
    out = []
    for core in range(8):
        k = core
        sh = lambda gs, kind: np.ascontiguousarray(
            np.stack([g[k * GROWS[kind] // 8:(k + 1) * GROWS[kind] // 8] for g in gs], axis=0), dtype=f32)
        out.append(dict(wsh_ffn=sh(ffn_g, "ffn"), wsh_mix=sh(mix_g, "mix"), wsh_ple=sh(ple_g, "ple")))
    return out


def _prep_gains(inp, layers, with_final):
    cols = []
    gm = lambda v: np.asarray(v, np.float32).reshape(-1, 128).T
    for l in layers:
        for name in ("ffn1_norm", "mix_norm", "q_lat_norm", "kv_lat_norm", "sb_out_norm", "mla_out_norm",
                     "ffn2_norm", "ple_norm"):
            cols.append(gm(inp[name][l]))
    cols.append(gm(inp["final_norm"]) if with_final else np.ones((128, 8), np.float32))
    return np.ascontiguousarray(np.concatenate(cols, axis=1))


def _prep_consts(c):
    m = np.zeros((12, 128, 128), np.float32)
    k = np.arange(128)[:, None]
    q = np.arange(128)[None, :]
    m[0] = 1.0
    m[1] = np.eye(128, dtype=np.float32)
    m[2] = np.where(k >= q, -8.0, 0.0)
    m[3] = -8.0
    for r in range(4):
        if r < c:
            sb = np.zeros((128, 128), np.float32)
            ml = np.zeros((128, 128), np.float32)
        elif r == c:
            sb = np.where(k < q, 0.0, NEG)
            ml = np.where(k <= q, 0.0, NEG)
        else:
            sb = np.full((128, 128), NEG, np.float32)
            ml = np.full((128, 128), NEG, np.float32)
        m[4 + r] = sb
        m[8 + r] = ml
    return np.ascontiguousarray(m.transpose(1, 0, 2))


def _prep_rope(c):
    pos = _positions(c).astype(np.float32)
    half = 32
    inv = (np.float32(10000.0) ** (-np.arange(half, dtype=np.float32) / np.float32(half))).astype(np.float32)
    ang = (pos[:, None] * inv[None, :]).astype(np.float32)
    cos = np.cos(ang).astype(np.float32).T
    sin = np.sin(ang).astype(np.float32).T
    out = np.zeros((64, 2, T), np.float32)
    out[0:32, 0], out[32:64, 0] = cos, cos
    out[0:32, 1], out[32:64, 1] = -sin, sin
    return out


def _to_fm(a):
    Tn, Fn = a.shape
    return np.ascontiguousarray(a.T.reshape(Fn // 128, 128, Tn).transpose(1, 0, 2))


def _from_fm(a):
    return a.transpose(1, 0, 2).reshape(-1, a.shape[2]).T


_PROGS = {}


def _get_prog(L, final):
    key = (L, final)
    if key not in _PROGS:
        _PROGS[key] = build_program(L, final)[0]
    return _PROGS[key]


def _run(hfm, inp, layers, final):
    L = len(layers)
    nc = _get_prog(L, final)
    W = _prep_weights(inp, layers)
    gains = _prep_gains(inp, layers, final)
    p = np.asarray(inp["p"], np.float32)
    in_maps = []
    for core in range(8):
        b, c = core // 4, core % 4
        pos = _positions(c)
        pT = np.stack([_to_fm(p[l, b][pos]) for l in layers], axis=0)
        m = dict(xT=hfm[core], pT=np.ascontiguousarray(pT), gains=gains, cmat=_prep_consts(c), rope=_prep_rope(c))
        m.update(W[core])
        in_maps.append(m)
    if TRACE:
        res = run_bass_kernel_spmd(nc, in_maps, core_ids=list(range(8)), trace=True)
        print("EXEC_TIME_NS", res.exec_time_ns)
    else:
        res = run_bass_kernel_spmd(nc, in_maps, core_ids=list(range(8)))
    if DEBUG:
        LAST["dbg"] = [np.asarray(res.results[core]["dbg"], np.float32) for core in range(8)]
    return [np.asarray(res.results[core]["outT"], np.float32) for core in range(8)]


FUSED = True


def kernel(**inputs):
    x = np.asarray(inputs["x"], np.float32)
    hfm = []
    for core in range(8):
        b, c = core // 4, core % 4
        hfm.append(_to_fm(x[b][_positions(c)]))
    if FUSED:
        outs = _run(hfm, inputs, [0, 1], True)
    else:
        outs = _run(hfm, inputs, [0], False)
        outs = _run([np.ascontiguousarray(o) for o in outs], inputs, [1], True)
    out = np.empty((2, 8192, D), np.float32)
    for core in range(8):
        b, c = core // 4, core % 4
        out[b, _positions(c)] = _from_fm(outs[core])
    return out
```
